# Optimizing a Trainium2 kernel written in Bass

```python
import jax
import jax.numpy as jnp
import numpy as np

D_MODEL = 4096
BATCH = 1
SEQ = 16384
DEPTH = 4

CHUNK = 64
MEM_LEN = 256
D_MIX = D_MODEL
CONV_WIDTH = D_MIX // 4
ATT_WIDTH = D_MIX // 2
POOL_WIDTH = D_MIX - CONV_WIDTH - ATT_WIDTH
CONV_HEADS = 8
CONV_KSIZE = 3
ATT_HEAD_DIM = 128
ATT_HEADS = ATT_WIDTH // ATT_HEAD_DIM
LEFT_CHUNKS = 8
BAND_CHUNKS = LEFT_CHUNKS + 1
MAX_REL = 256
POOL_WINDOWS = (2, 4, 8, 16)
POOL_GROUPS = len(POOL_WINDOWS)
POOL_GROUP_WIDTH = POOL_WIDTH // POOL_GROUPS
X_HEADS = 4
X_HEAD_DIM = 256
X_WIDTH = X_HEADS * X_HEAD_DIM
D_FF = ((8 * D_MODEL + 3 * 256 - 1) // (3 * 256)) * 256
D_IN = 3 * CONV_WIDTH + 3 * ATT_WIDTH + POOL_WIDTH
SPLITS = (CONV_WIDTH, 2 * CONV_WIDTH, 3 * CONV_WIDTH,
          3 * CONV_WIDTH + ATT_WIDTH, 3 * CONV_WIDTH + 2 * ATT_WIDTH,
          3 * CONV_WIDTH + 3 * ATT_WIDTH)
RMS_EPS = 1e-6
NEG_INF = -1e30

kernel_name = 'hybrid_streaming_encoder'


def rmsnorm(x, g):
    xf = x.astype(jnp.float32)
    y = xf * jax.lax.rsqrt(jnp.mean(xf * xf, axis=-1, keepdims=True) + RMS_EPS)
    return (y * g.astype(jnp.float32)).astype(x.dtype)


def short_conv_mixer(b, c, h, w_conv):
    u = c * h
    s = u.shape[1]
    up = jnp.pad(u, ((0, 0), (CONV_KSIZE - 1, 0), (0, 0)))
    conv = up[:, 0:s] * w_conv[0]
    for k in range(1, CONV_KSIZE):
        conv = conv + up[:, k:k + s] * w_conv[k]
    return b * conv


def chunk_attention(q, k, v, rel_bias):
    bsz, s, h, dh = q.shape
    nc = s // CHUNK
    qc = q.reshape(bsz, nc, CHUNK, h, dh)
    pad = ((0, 0), (LEFT_CHUNKS, 0), (0, 0), (0, 0), (0, 0))
    kp = jnp.pad(k.reshape(bsz, nc, CHUNK, h, dh), pad)
    vp = jnp.pad(v.reshape(bsz, nc, CHUNK, h, dh), pad)
    scale = dh ** -0.5
    scores = jnp.concatenate(
        [jnp.einsum('bnqhd,bnkhd->bnhqk', qc, kp[:, j:j + nc],
                    preferred_element_type=jnp.float32) for j in range(BAND_CHUNKS)], axis=-1) * scale
    q_pos = np.arange(CHUNK)[:, None]
    k_pos = np.arange(BAND_CHUNKS * CHUNK)[None, :] - LEFT_CHUNKS * CHUNK
    rel_idx = np.clip(q_pos - k_pos, -MAX_REL, MAX_REL) + MAX_REL
    bias = rel_bias[:, rel_idx].astype(jnp.float32)
    key_chunk = np.arange(nc)[:, None] - LEFT_CHUNKS + np.repeat(np.arange(BAND_CHUNKS), CHUNK)[None, :]
    valid = jnp.asarray(key_chunk >= 0)
    scores = jnp.where(valid[None, :, None, None, :], scores + bias, NEG_INF)
    p = jax.nn.softmax(scores, axis=-1).astype(v.dtype)
    p = p.reshape(bsz, nc, h, CHUNK, BAND_CHUNKS, CHUNK)
    out = jnp.einsum('bnhqk,bnkhd->bnqhd', p[:, :, :, :, 0, :], vp[:, 0:nc])
    for j in range(1, BAND_CHUNKS):
        out = out + jnp.einsum('bnhqk,bnkhd->bnqhd', p[:, :, :, :, j, :], vp[:, j:j + nc])
    return out.reshape(bsz, s, h * dh)


def pool_mixer(u, w_pool, pool_scale):
    bsz, s, _ = u.shape
    uf = u.astype(jnp.float32)
    cs = jnp.pad(jnp.cumsum(uf, axis=1), ((0, 0), (1, 0), (0, 0)))
    t = np.arange(s)
    groups = []
    for g, w in enumerate(POOL_WINDOWS):
        lo, hi = g * POOL_GROUP_WIDTH, (g + 1) * POOL_GROUP_WIDTH
        cg = cs[:, :, lo:hi]
        start = jnp.pad(cg[:, :s - w + 1], ((0, 0), (w - 1, 0), (0, 0)))
        count = jnp.asarray(np.minimum(t + 1, w).astype(np.float32))[None, :, None]
        groups.append((cg[:, 1:] - start) / count - uf[:, :, lo:hi])
    pooled = jnp.stack(groups, axis=2).astype(u.dtype)
    mixed = jnp.einsum('bsgc,gcd->bsgd', pooled, w_pool)
    return mixed.reshape(bsz, s, POOL_WIDTH) * pool_scale


def memory_cross_attention(xn, memn, w_xq, w_xkv, w_xo):
    bsz, s, _ = xn.shape
    q = (xn @ w_xq).reshape(bsz, s, X_HEADS, X_HEAD_DIM)
    kv = (memn @ w_xkv).reshape(bsz, MEM_LEN, 2, X_HEADS, X_HEAD_DIM)
    k, v = kv[:, :, 0], kv[:, :, 1]
    scores = jnp.einsum('bshd,bmhd->bhsm', q, k, preferred_element_type=jnp.float32) * (X_HEAD_DIM ** -0.5)
    p = jax.nn.softmax(scores, axis=-1).astype(v.dtype)
    o = jnp.einsum('bhsm,bmhd->bshd', p, v).reshape(bsz, s, X_WIDTH)
    return o @ w_xo


def swiglu(xn, w_gate, w_up, w_down):
    return (jax.nn.silu(xn @ w_gate) * (xn @ w_up)) @ w_down


def setup_inputs(seed: int = 0) -> dict:
    key = jax.random.key(seed)
    ks = jax.random.split(key, 24)
    f32 = jnp.float32

    def w(k, shape, fan_in):
        return jax.random.normal(k, shape, f32) * (fan_in ** -0.5)

    def gain(k, shape):
        return 1.0 + 0.05 * jax.random.normal(k, shape, f32)

    return {
        'x': jax.random.normal(ks[0], (BATCH, SEQ, D_MODEL), f32),
        'mem': jax.random.normal(ks[1], (BATCH, MEM_LEN, D_MODEL), f32),
        'w_in': w(ks[2], (DEPTH, D_MODEL, D_IN), D_MODEL),
        'w_conv': w(ks[3], (DEPTH, CONV_KSIZE, CONV_WIDTH), CONV_KSIZE),
        'rel_bias': 0.1 * jax.random.normal(ks[4], (DEPTH, ATT_HEADS, 2 * MAX_REL + 1), f32),
        'w_pool': w(ks[5], (DEPTH, POOL_GROUPS, POOL_GROUP_WIDTH, POOL_GROUP_WIDTH), POOL_GROUP_WIDTH),
        'pool_scale': 1.0 + 0.1 * jax.random.normal(ks[6], (DEPTH, POOL_WIDTH), f32),
        'w_out': w(ks[7], (DEPTH, D_MIX, D_MODEL), D_MIX),
        'w_xq': w(ks[8], (DEPTH, D_MODEL, X_WIDTH), D_MODEL),
        'w_xkv': w(ks[9], (DEPTH, D_MODEL, 2 * X_WIDTH), D_MODEL),
        'w_xo': w(ks[10], (DEPTH, X_WIDTH, D_MODEL), X_WIDTH),
        'w_gate': w(ks[11], (DEPTH, D_MODEL, D_FF), D_MODEL),
        'w_up': w(ks[12], (DEPTH, D_MODEL, D_FF), D_MODEL),
        'w_down': w(ks[13], (DEPTH, D_FF, D_MODEL), D_FF),
        'g_mix_pre': gain(ks[14], (DEPTH, D_MODEL)),
        'g_mix_post': gain(ks[15], (DEPTH, D_MODEL)),
        'g_x_pre': gain(ks[16], (DEPTH, D_MODEL)),
        'g_x_post': gain(ks[17], (DEPTH, D_MODEL)),
        'g_ffn_pre': gain(ks[18], (DEPTH, D_MODEL)),
        'g_ffn_post': gain(ks[19], (DEPTH, D_MODEL)),
        'g_mem': gain(ks[20], (D_MODEL,)),
    }


def reference(x, mem, w_in, w_conv, rel_bias, w_pool, pool_scale, w_out,
              w_xq, w_xkv, w_xo, w_gate, w_up, w_down,
              g_mix_pre, g_mix_post, g_x_pre, g_x_post, g_ffn_pre, g_ffn_post, g_mem):
    bsz, s, _ = x.shape
    memn = rmsnorm(mem, g_mem)
    for l in range(DEPTH):
        xn = rmsnorm(x, g_mix_pre[l])
        proj = xn @ w_in[l]
        b, c, h, q, k, v, u = jnp.split(proj, SPLITS, axis=-1)
        y_conv = short_conv_mixer(b, c, h, w_conv[l])
        y_att = chunk_attention(q.reshape(bsz, s, ATT_HEADS, ATT_HEAD_DIM),
                                k.reshape(bsz, s, ATT_HEADS, ATT_HEAD_DIM),
                                v.reshape(bsz, s, ATT_HEADS, ATT_HEAD_DIM), rel_bias[l])
        y_pool = pool_mixer(u, w_pool[l], pool_scale[l])
        mix = jnp.concatenate([y_conv, y_att, y_pool], axis=-1) @ w_out[l]
        x = x + rmsnorm(mix, g_mix_post[l])
        xc = memory_cross_attention(rmsnorm(x, g_x_pre[l]), memn, w_xq[l], w_xkv[l], w_xo[l])
        x = x + rmsnorm(xc, g_x_post[l])
        f = swiglu(rmsnorm(x, g_ffn_pre[l]), w_gate[l], w_up[l], w_down[l])
        x = x + rmsnorm(f, g_ffn_post[l])
    return x
```

```python
import bisect
from contextlib import ExitStack

import numpy as np
import concourse.bass as bass
import concourse.mybir as mybir
from concourse.bass_utils import run_bass_kernel_spmd

F32, BF16 = mybir.dt.float32, mybir.dt.bfloat16
AF = mybir.ActivationFunctionType
ALU = mybir.AluOpType
T = 512
NCORES = 8
NEG = -30000.0
EPS = 1e-6
SAME_ENGINE_SYNC = True
import os as _os
_STOP = _os.environ.get("KSTOP", "")


class _Stop(Exception):
    pass


_CNT = {}


def chk(name):
    if not _STOP:
        return
    nm, _, k = _STOP.partition(":")
    if name == nm:
        _CNT[name] = _CNT.get(name, 0) + 1
        if _CNT[name] >= int(k or 1):
            raise _Stop()


class Cfg:
    def __init__(s, D=4096, NCONV=8, H=16, PGC=2, XH=4, DFF=11008, MEM=256, NT_OWN=4, L=4):
        s.D, s.KC = D, D // 128
        s.NCONV, s.CW = NCONV, NCONV * 128
        s.H, s.AW, s.HG = H, H * 128, H // 4
        s.PGC, s.PGW, s.NPOOL, s.PW = PGC, PGC * 128, 4 * PGC, 4 * PGC * 128
        assert s.NPOOL == NCONV and s.CW + s.AW + s.PW == D and H % 4 == 0
        s.XH, s.XW, s.XC = XH, XH * 256, XH * 2
        s.DFF, s.FC = DFF, DFF // 128
        assert DFF % 256 == 0
        s.DIN = 3 * s.CW + 3 * s.AW + s.PW
        s.MEM = MEM
        s.NT_OWN, s.L = NT_OWN, L
        s.G = 4
        s.NSLOT = 4


class Op:
    __slots__ = ("eng", "fn", "deps", "dma", "cons", "sig", "sem", "val", "prev")

    def __init__(s, eng, fn, dma):
        s.eng, s.fn, s.dma = eng, fn, dma
        s.deps = set()
        s.cons = False
        s.sig = 0
        s.sem = None
        s.val = 0
        s.prev = 0


class IMap:
    def __init__(s):
        s.b = [0]
        s.w = [None]
        s.r = [None]

    def _split(s, pos):
        i = bisect.bisect_right(s.b, pos) - 1
        if s.b[i] == pos:
            return i
        s.b.insert(i + 1, pos)
        s.w.insert(i + 1, s.w[i])
        s.r.insert(i + 1, dict(s.r[i]) if s.r[i] else None)
        return i + 1

    def span(s, lo, hi):
        i = s._split(lo)
        j = s._split(hi)
        return i, j


COMPUTE = ("pe", "act", "dve")
QUEUES = ("sp", "pool")
EPOCH = 30000
NPOOLSEM = 16


class Prog:
    def __init__(s):
        s.ops = {e: [] for e in COMPUTE + QUEUES}
        s.maps = {}

    def op(s, eng, fn, R=(), W=(), dma=False):
        o = Op(eng, fn, dma)
        deps = o.deps
        for (a, lo, hi) in R:
            m = s.maps.get(a)
            if m is None:
                m = s.maps[a] = IMap()
            i, j = m.span(lo, hi)
            for k in range(i, j):
                if m.w[k] is not None:
                    deps.add(m.w[k])
                if m.r[k] is None:
                    m.r[k] = {}
                m.r[k][eng] = o
        for (a, lo, hi) in W:
            m = s.maps.get(a)
            if m is None:
                m = s.maps[a] = IMap()
            i, j = m.span(lo, hi)
            for k in range(i, j):
                if m.w[k] is not None:
                    deps.add(m.w[k])
                if m.r[k]:
                    deps.update(m.r[k].values())
                m.w[k] = o
                m.r[k] = None
        deps.discard(o)
        for d in deps:
            if d.eng == "pe" and eng == "pe":
                continue
            if (not SAME_ENGINE_SYNC) and d.eng == eng and eng in COMPUTE:
                continue
            d.cons = True
        s.ops[eng].append(o)
        return o

    def emit(s, nc, es):
        nsig = {}
        for e in COMPUTE:
            idx = 0
            for o in s.ops[e]:
                if o.cons:
                    idx += 1
                    o.sig = idx
            nsig[e] = idx
        csem = {e: [es.enter_context(nc.semaphore(f"c_{e}_{k}")) for k in range(nsig[e] // EPOCH + 1)]
                for e in COMPUTE}
        qsem = {q: [es.enter_context(nc.semaphore(f"q_{q}_{k}")) for k in range(NPOOLSEM)] for q in QUEUES}
        final = {}
        for q in QUEUES:
            for j, o in enumerate(s.ops[q]):
                assert o.dma
                o.sem = (q, j % NPOOLSEM)
                o.val = 16 * (j // NPOOLSEM + 1)
                o.prev = 16 * (j // NPOOLSEM)
                final[o.sem] = o.val
        block = es.enter_context(nc.Block())

        def run(e, eng):
            known = {}
            for o in s.ops[e]:
                waits = {}
                for d in o.deps:
                    if d.dma:
                        key, need = d.sem, d.val
                    else:
                        if d.eng == e and (e == "pe" or not SAME_ENGINE_SYNC):
                            continue
                        key, need = d.eng, d.sig
                    if known.get(key, 0) >= need:
                        continue
                    if waits.get(key, 0) < need:
                        waits[key] = need
                if o.dma and o.prev > 0 and known.get(o.sem, 0) < o.prev and waits.get(o.sem, 0) < o.prev:
                    waits[o.sem] = o.prev
                for key, need in waits.items():
                    if isinstance(key, tuple):
                        eng.wait_ge(qsem[key[0]][key[1]], need)
                    else:
                        eng.wait_ge(csem[key][(need - 1) // EPOCH], (need - 1) % EPOCH + 1)
                    known[key] = need
                ins = o.fn(eng)
                if o.dma:
                    ins.then_inc(qsem[o.sem[0]][o.sem[1]], 16)
                elif o.cons:
                    ins.then_inc(csem[e][(o.sig - 1) // EPOCH], 1)
            if e == "sp":
                for key, need in final.items():
                    if known.get(key, 0) < need:
                        eng.wait_ge(qsem[key[0]][key[1]], need)

        @block.tensor
        def _(eng):
            run("pe", eng)

        @block.scalar
        def _(eng):
            run("act", eng)

        @block.vector
        def _(eng):
            run("dve", eng)

        @block.sync
        def _(eng):
            run("sp", eng)

        @block.gpsimd
        def _(eng):
            run("pool", eng)


class Buf:
    def __init__(s, arena, name, off, a, n):
        s.arena, s.name, s.off, s.a, s.n = arena, name, off, a, n
        s.size = (a or 1) * n
        s._v = None

    @property
    def v(s):
        if s._v is None:
            base = s.arena.t[:, s.off:s.off + s.size]
            s._v = base.rearrange("p (a n) -> p a n", n=s.n) if s.a is not None else base
        return s._v

    def ap(s, a=None, n0=0, n1=None, p0=0, p1=128):
        n1 = s.n if n1 is None else n1
        if s.a is None:
            return s.v[p0:p1, n0:n1]
        if isinstance(a, tuple):
            return s.v[p0:p1, a[0]:a[1], n0:n1]
        return s.v[p0:p1, a, n0:n1]

    def rg(s, a=None, n0=0, n1=None):
        n1 = s.n if n1 is None else n1
        if s.a is None:
            return [(s.arena.name, s.off + n0, s.off + n1)]
        if isinstance(a, tuple):
            a0, a1 = a
        else:
            a0, a1 = a, a + 1
        if n0 == 0 and n1 == s.n:
            return [(s.arena.name, s.off + a0 * s.n, s.off + a1 * s.n)]
        return [(s.arena.name, s.off + k * s.n + n0, s.off + k * s.n + n1) for k in range(a0, a1)]


class Arena:
    def __init__(s, name, t, size):
        s.name, s.t, s.size, s.top = name, t, size, 0
        s.hi = 0

    def alloc(s, name, a, n):
        b = Buf(s, name, s.top, a, n)
        s.top += b.size
        s.top = (s.top + 15) // 16 * 16
        s.hi = max(s.hi, s.top)
        assert s.size is None or s.top <= s.size, (s.name, name, s.top, s.size)
        return b


def dreg(name, ncols, r0, r1, c0, c1):
    if c0 == 0 and c1 == ncols:
        return [(name, r0 * ncols, r1 * ncols)]
    out = []
    r = r0
    while r < r1:
        rb = min(r1, (r // 128 + 1) * 128)
        out.append((name, (r // 128) * 128 * ncols + c0 * 128, (r // 128) * 128 * ncols + c1 * 128))
        r = rb
    return out


def build_program(cfg, layers, first_layer_is_input=True, last_writes_output=True):
    c = cfg
    LP = len(layers)
    NLOC = c.NT_OWN + LP
    NTOK = NLOC * T
    OWN0 = LP
    KC, FC, XC, G = c.KC, c.FC, c.XC, c.G
    nc = bass.Bass("TRN2", target_bir_lowering=False)

    def din(name, shape, dt=F32):
        return nc.dram_tensor(name, list(shape), dt, kind="ExternalInput").ap()

    x_in = din("x_in", [c.D, NTOK])
    memT = din("memT", [c.D, c.MEM])
    w_in = din("w_in", [LP, c.D, c.DIN])
    w_gu = din("w_gu", [LP, c.D, 2 * c.DFF])
    w_down = din("w_down", [LP, c.DFF, c.D])
    w_out = din("w_out", [LP, c.D, c.D])
    w_xq = din("w_xq", [LP, c.D, c.XW])
    w_xkv = din("w_xkv", [LP, c.D, 2 * c.XW])
    w_xo = din("w_xo", [LP, c.XW, c.D])
    w_pool = din("w_pool", [LP, 4, c.PGW, c.PGW])
    biasT = din("biasT", [LP, c.H, 128, 640])
    NSM = LP * 6 * KC + KC + LP * c.NCONV * 3 + LP * c.NPOOL + 2 + 64
    small = din("small", [128, NSM])
    x_out = nc.dram_tensor("x_out", [c.D, c.NT_OWN * T], F32, kind="ExternalOutput").ap()
    xs = nc.dram_tensor("xs", [c.D, NTOK], F32).ap()
    ys = nc.dram_tensor("ys", [c.D, T], F32).ap()
    kscr = nc.dram_tensor("kscr", [c.AW, NTOK], BF16).ap()
    vscr = nc.dram_tensor("vscr", [NTOK, c.AW], BF16).ap()
    mkscr = nc.dram_tensor("mkscr", [LP, c.XW, c.MEM], BF16).ap()
    mvscr = nc.dram_tensor("mvscr", [LP, c.MEM, c.XW], BF16).ap()

    es = ExitStack()
    tps = es.enter_context(nc.psum_tensor("PS", [128, 8 * 512], F32))
    A16 = Arena("A16", None, None)
    A32 = Arena("A32", None, None)
    APS = Arena("PS", tps, 8 * 512)
    bank = [APS.alloc(f"bank{i}", None, 512) for i in range(8)]
    scp = [Buf(APS, "scp0", 0, None, 1024), Buf(APS, "scp1", 1024, None, 1024)]

    P = Prog()

    xg = A16.alloc("xg", KC, T)
    wring = [A16.alloc(f"w{i}", G, T) for i in range(c.NSLOT)]
    ones16 = A16.alloc("ones16", None, 128)
    wpool = A16.alloc("wpool", 4 * c.PGC, c.PGW)
    m16 = A16.top
    sm = A32.alloc("small", None, NSM)
    ones32 = A32.alloc("ones32", None, 128)
    epsb = A32.alloc("eps", None, 16)
    conv_hist = A32.alloc("conv_hist", c.NCONV, 16)
    pool_hist = A32.alloc("pool_hist", c.NPOOL, 16)
    rstd_row = A32.alloc("rstd_row", None, T)
    rstd_col = A32.alloc("rstd_col", None, 16)
    acc = A32.alloc("acc", None, T)
    sqt = [A32.alloc(f"sqt{i}", None, T) for i in range(2)]
    xld = [A32.alloc(f"xld{i}", None, T) for i in range(2)]
    yld = [A32.alloc(f"yld{i}", None, T) for i in range(2)]
    xnw = [A32.alloc(f"xnw{i}", None, T) for i in range(2)]
    ybuf = [A32.alloc(f"ybuf{i}", None, T) for i in range(2)]
    tmpa = A32.alloc("tmpa", None, T)
    m32 = A32.top

    o_g = 0
    o_gmem = LP * 6 * KC
    o_wc = o_gmem + KC
    o_ps = o_wc + LP * c.NCONV * 3
    o_neg = o_ps + LP * c.NPOOL
    o_hv = o_neg + 1
    o_corr = o_hv + 1

    def sm_ap(off, n=1):
        return sm.ap(n0=off, n1=off + n)

    def gain_ap(li, which, kc):
        return sm_ap(o_g + (li * 6 + which) * KC + kc)

    SM_RG = sm.rg()

    A16.top = m16
    mix = A16.alloc("mix", KC, T)
    Qb = A16.alloc("Q", 4, T)
    Kb = A16.alloc("K", 4, 2 * T)
    Vb = A16.alloc("V", 8, T)
    pooled = A16.alloc("pooled", c.NPOOL, T)
    PT = [A16.alloc(f"PT{i}", None, 640) for i in range(2)]
    A16.top = m16
    qx = A16.alloc("qx", XC, T)
    ox = A16.alloc("ox", XC, T)
    memK = A16.alloc("memK", XC, c.MEM)
    memV = A16.alloc("memV", c.MEM // 128, c.XW)
    PTx = [A16.alloc(f"PTx{i}", c.MEM // 128, T) for i in range(2)]
    xgm = A16.alloc("xgm", KC, c.MEM)
    A16.top = m16
    hb = A16.alloc("h", FC, T)

    A32.top = m32
    bT = A32.alloc("bT", None, T)
    cT = A32.alloc("cT", None, T)
    hT = A32.alloc("hT", None, T)
    u2 = A32.alloc("u2", None, T + 16)
    cacc = A32.alloc("cacc", None, T)
    ubuf = A32.alloc("ubuf", None, T + 16)
    sA = A32.alloc("sA", None, T + 16)
    sB = A32.alloc("sB", None, T + 16)
    bias_sb = [A32.alloc(f"bias{i}", None, 640) for i in range(2)]
    sct = A32.alloc("sct", None, 640)
    rec = A32.alloc("rec", None, T)
    A32.top = m32
    t1 = [A32.alloc(f"t1{i}", None, T) for i in range(2)]
    sg = [A32.alloc(f"sg{i}", None, T) for i in range(2)]
    t2 = [A32.alloc(f"t2{i}", None, T) for i in range(2)]
    A16.t = es.enter_context(nc.sbuf_tensor("A16", [128, A16.hi], BF16))
    A32.t = es.enter_context(nc.sbuf_tensor("A32", [128, A32.hi], F32))

    state = {"slot": 0, "set": 0, "xl": 0, "yl": 0, "xn": 0, "yb": 0, "sq": 0}

    def rot(key, n):
        v = state[key]
        state[key] = (v + 1) % n
        return v

    for ar in (A16, A32):
        for lo in range(0, ar.hi, 8192):
            hi_ = min(ar.hi, lo + 8192)
            P.op("dve", (lambda e, ar=ar, lo=lo, hi_=hi_: e.memset(ar.t[:, lo:hi_], 0.0)), W=[(ar.name, lo, hi_)])
    P.op("dve", lambda e: e.memset(ones32.ap(), 1.0), W=ones32.rg())
    P.op("dve", lambda e: e.memset(ones16.ap(), 1.0), W=ones16.rg())
    P.op("dve", lambda e: e.memset(epsb.ap(), EPS), W=epsb.rg())
    P.op("dve", lambda e: e.memset(conv_hist.ap((0, c.NCONV)), 0.0), W=conv_hist.rg((0, c.NCONV)))
    P.op("dve", lambda e: e.memset(pool_hist.ap((0, c.NPOOL)), 0.0), W=pool_hist.rg((0, c.NPOOL)))
    P.op("sp", lambda e: e.dma_start(out=sm.ap(), in_=small), R=[("small_d", 0, 1)], W=SM_RG, dma=True)

    def wview(w2d):
        return w2d.rearrange("(c p) n -> p c n", p=128)

    def dense(wname, w2d, Kc, col0, ncols, lhs_fn, rhs_fn, nout, evac, order="fm"):
        wv = wview(w2d)
        ncolW = w2d.shape[1]
        for cg in range(col0, col0 + ncols, 512):
            cw = min(512, col0 + ncols - cg)
            bset = state["set"]
            state["set"] ^= 1
            banks = [bank[bset * 4 + j] for j in range(4)]
            nj = cw // 128 if order == "fm" else nout
            for kt in range(0, Kc, G):
                g = min(G, Kc - kt)
                si = rot("slot", c.NSLOT)
                ws = wring[si]
                P.op("pool",
                     (lambda e, ws=ws, g=g, kt=kt, cg=cg, cw=cw: e.dma_start(
                         out=ws.ap((0, g), 0, cw), in_=wv[:, kt:kt + g, cg:cg + cw])),
                     R=[(wname, 0, 1)], W=ws.rg((0, g), 0, cw), dma=True)
                for kc in range(g):
                    k = kt + kc
                    for j in range(nj):
                        if order == "fm":
                            lhs, lhs_rg = ws.ap(kc, j * 128, (j + 1) * 128), ws.rg(kc, j * 128, (j + 1) * 128)
                            rhs, rhs_rg, n = rhs_fn(k)
                            out_ap = banks[j].ap(n0=0, n1=n)
                            out_rg = banks[j].rg(n0=0, n1=n)
                        else:
                            lhs, lhs_rg = lhs_fn(k, j)
                            rhs, rhs_rg = ws.ap(kc, 0, cw), ws.rg(kc, 0, cw)
                            out_ap = banks[j].ap(n0=0, n1=cw)
                            out_rg = banks[j].rg(n0=0, n1=cw)
                        P.op("pe",
                             (lambda e, o=out_ap, l=lhs, r=rhs, st=(k == 0), sp=(k == Kc - 1):
                              e.matmul(o, l, r, start=st, stop=sp)),
                             R=lhs_rg + rhs_rg, W=out_rg)
            evac((cg - col0) // 128, nj, banks, cw)

    def stats_to_rstd(n, row=rstd_row, want_col=False):
        b7 = bank[7]
        P.op("pe", lambda e: e.matmul(b7.ap(n0=0, n1=n), ones32.ap(), acc.ap(n0=0, n1=n), start=True, stop=True),
             R=ones32.rg() + acc.rg(n0=0, n1=n), W=b7.rg(n0=0, n1=n))
        P.op("act", lambda e: e.activation(tmpa.ap(n0=0, n1=n), b7.ap(n0=0, n1=n), AF.Sqrt,
                                           bias=epsb.ap(n0=0, n1=1), scale=1.0 / c.D),
             R=b7.rg(n0=0, n1=n) + epsb.rg(), W=tmpa.rg(n0=0, n1=n))
        P.op("dve", lambda e: e.reciprocal(row.ap(n0=0, n1=n), tmpa.ap(n0=0, n1=n)),
             R=tmpa.rg(n0=0, n1=n), W=row.rg(n0=0, n1=n))
        if want_col:
            b6 = bank[6]
            nb = n // 128
            for tb in range(nb):
                P.op("pe", (lambda e, tb=tb: e.matmul(b6.ap(n0=tb, n1=tb + 1),
                                                       row.ap(n0=tb * 128, n1=(tb + 1) * 128, p0=0, p1=1),
                                                       ones32.ap(n0=0, n1=1, p0=0, p1=1), start=True, stop=True)),
                     R=row.rg(n0=tb * 128, n1=(tb + 1) * 128) + ones32.rg(), W=b6.rg(n0=tb, n1=tb + 1))
            P.op("dve", lambda e: e.tensor_copy(rstd_col.ap(n0=0, n1=nb), b6.ap(n0=0, n1=nb)),
                 R=b6.rg(n0=0, n1=nb), W=rstd_col.rg(n0=0, n1=nb))

    def acc_sq(src_ap, src_rg, first, n=T):
        if first:
            P.op("act", lambda e: e.activation(acc.ap(n0=0, n1=n), src_ap, AF.Square),
                 R=src_rg, W=acc.rg(n0=0, n1=n))
        else:
            q = sqt[rot("sq", 2)]
            P.op("act", lambda e: e.activation(q.ap(n0=0, n1=n), src_ap, AF.Square), R=src_rg, W=q.rg(n0=0, n1=n))
            P.op("dve", lambda e: e.tensor_tensor(acc.ap(n0=0, n1=n), acc.ap(n0=0, n1=n), q.ap(n0=0, n1=n), ALU.add),
                 R=acc.rg(n0=0, n1=n) + q.rg(n0=0, n1=n), W=acc.rg(n0=0, n1=n))

    def xsrc(layer_pos, ti):
        if layer_pos == 0 and first_layer_is_input:
            return x_in, "x_in", NTOK, ti * T
        return xs, "xs", NTOK, ti * T

    def prologue(li, ti, src, want_col):
        xt, xname, ncol, t0 = src
        xv = xt.rearrange("(c p) t -> p c t", p=128)
        for kc in range(KC):
            xb = xld[rot("xl", 2)]
            P.op("sp", (lambda e, xb=xb, kc=kc: e.dma_start(out=xb.ap(), in_=xv[:, kc, t0:t0 + T])),
                 R=dreg(xname, ncol // 128, kc * 128, (kc + 1) * 128, t0 // 128, (t0 + T) // 128), W=xb.rg(), dma=True)
            P.op("act", (lambda e, xb=xb, kc=kc: e.activation(xg.ap(kc), xb.ap(), AF.Copy,
                                                               scale=gain_ap(li, 0, kc))),
                 R=xb.rg() + SM_RG, W=xg.rg(kc))
            acc_sq(xb.ap(), xb.rg(), kc == 0)
        stats_to_rstd(T, want_col=want_col)

    def post_evac(kind):
        ysv = ys.rearrange("(c p) t -> p c t", p=128)

        def ev(c0, nj, banks, cw):
            for j in range(nj):
                oc = c0 + j
                yb = ybuf[rot("yb", 2)]
                bj = banks[j]
                P.op("act", (lambda e, yb=yb, bj=bj: e.activation(yb.ap(), bj.ap(), AF.Copy)),
                     R=bj.rg(), W=yb.rg())
                acc_sq(yb.ap(), yb.rg(), oc == 0)
                P.op("sp", (lambda e, yb=yb, oc=oc: e.dma_start(out=ysv[:, oc, :], in_=yb.ap())),
                     R=yb.rg(), W=dreg("ys", T // 128, oc * 128, (oc + 1) * 128, 0, T // 128), dma=True)
        return ev

    def tail(li, ti, src, dst, g_post, g_next):
        stats_to_rstd(T, row=rstd_row)
        xt, xname, ncol, t0 = src
        dt_, dname, dncol, d0 = dst
        xv = xt.rearrange("(c p) t -> p c t", p=128)
        dv = dt_.rearrange("(c p) t -> p c t", p=128)
        ysv = ys.rearrange("(c p) t -> p c t", p=128)
        loads = []

        def issue(kc):
            xb = xld[rot("xl", 2)]
            yb = yld[rot("yl", 2)]
            P.op("sp", (lambda e, xb=xb, kc=kc: e.dma_start(out=xb.ap(), in_=xv[:, kc, t0:t0 + T])),
                 R=dreg(xname, ncol // 128, kc * 128, (kc + 1) * 128, t0 // 128, (t0 + T) // 128), W=xb.rg(), dma=True)
            P.op("sp", (lambda e, yb=yb, kc=kc: e.dma_start(out=yb.ap(), in_=ysv[:, kc, :])),
                 R=dreg("ys", T // 128, kc * 128, (kc + 1) * 128, 0, T // 128), W=yb.rg(), dma=True)
            loads.append((xb, yb))

        issue(0)
        for kc in range(KC):
            if kc + 1 < KC:
                issue(kc + 1)
            xb, yb = loads[kc]
            xn = xnw[rot("xn", 2)]
            P.op("dve", (lambda e, yb=yb: e.tensor_tensor(yb.ap(), yb.ap(), rstd_row.ap(), ALU.mult)),
                 R=yb.rg() + rstd_row.rg(), W=yb.rg())
            P.op("dve", (lambda e, yb=yb, xb=xb, xn=xn, kc=kc: e.scalar_tensor_tensor(
                xn.ap(), yb.ap(), gain_ap(li, g_post, kc), xb.ap(), ALU.mult, ALU.add)),
                R=yb.rg() + xb.rg() + SM_RG, W=xn.rg())
            P.op("sp", (lambda e, xn=xn, kc=kc: e.dma_start(out=dv[:, kc, d0:d0 + T], in_=xn.ap())),
                 R=xn.rg(), W=dreg(dname, dncol // 128, kc * 128, (kc + 1) * 128, d0 // 128, (d0 + T) // 128), dma=True)
            if g_next is not None:
                P.op("act", (lambda e, xn=xn, kc=kc: e.activation(xg.ap(kc), xn.ap(), AF.Copy,
                                                                   scale=gain_ap(li, g_next, kc))),
                     R=xn.rg() + SM_RG, W=xg.rg(kc))
                acc_sq(xn.ap(), xn.rg(), kc == 0)
        if g_next is not None:
            stats_to_rstd(T)

    def ev_scaled(dst_fn):
        def ev(c0, nj, banks, cw):
            for j in range(nj):
                d_ap, d_rg = dst_fn(c0 + j)
                bj = banks[j]
                P.op("dve", (lambda e, d_ap=d_ap, bj=bj: e.tensor_tensor(d_ap, bj.ap(), rstd_row.ap(), ALU.mult)),
                     R=bj.rg() + rstd_row.rg(), W=d_rg)
        return ev

    def mem_prologue():
        mv = memT.rearrange("(c p) t -> p c t", p=128)
        M = c.MEM
        for kc in range(KC):
            xb = xld[rot("xl", 2)]
            P.op("sp", (lambda e, xb=xb, kc=kc: e.dma_start(out=xb.ap(n0=0, n1=M), in_=mv[:, kc, :])),
                 R=[("memT", 0, 1)], W=xb.rg(n0=0, n1=M), dma=True)
            P.op("act", (lambda e, xb=xb, kc=kc: e.activation(xgm.ap(kc), xb.ap(n0=0, n1=M), AF.Copy,
                                                               scale=sm_ap(o_gmem + kc))),
                 R=xb.rg(n0=0, n1=M) + SM_RG, W=xgm.rg(kc))
            acc_sq(xb.ap(n0=0, n1=M), xb.rg(n0=0, n1=M), kc == 0, n=M)
        chk("mempro")
        stats_to_rstd(M, want_col=True)
        chk("memstats")
        for lp in range(LP):
            mkv = mkscr[lp].rearrange("(c p) t -> p c t", p=128)

            def evk(c0, nj, banks, cw, lp=lp, mkv=mkv):
                for j in range(nj):
                    oc = c0 + j
                    bj = banks[j]
                    P.op("dve", (lambda e, oc=oc, bj=bj: e.tensor_tensor(memK.ap(oc), bj.ap(n0=0, n1=M),
                                                                        rstd_row.ap(n0=0, n1=M), ALU.mult)),
                         R=bj.rg(n0=0, n1=M) + rstd_row.rg(n0=0, n1=M), W=memK.rg(oc))
                    P.op("sp", (lambda e, oc=oc: e.dma_start(out=mkv[:, oc, :], in_=memK.ap(oc))),
                         R=memK.rg(oc), W=[(f"mk{lp}", oc, oc + 1)], dma=True)
            dense("w_xkv", w_xkv[lp], KC, 0, c.XW, None, lambda k: (xgm.ap(k), xgm.rg(k), M), 4, evk, "fm")
            mvv = mvscr[lp].rearrange("(b p) n -> p b n", p=128)

            def evv(c0, nj, banks, cw, lp=lp, mvv=mvv):
                for tb in range(nj):
                    bj = banks[tb]
                    P.op("act", (lambda e, tb=tb, bj=bj, c0=c0, cw=cw: e.activation(
                        memV.ap(tb, c0 * 128, c0 * 128 + cw), bj.ap(n0=0, n1=cw), AF.Copy,
                        scale=rstd_col.ap(n0=tb, n1=tb + 1))),
                        R=bj.rg(n0=0, n1=cw) + rstd_col.rg(), W=memV.rg(tb, c0 * 128, c0 * 128 + cw))
                    P.op("sp", (lambda e, tb=tb, c0=c0, cw=cw: e.dma_start(
                        out=mvv[:, tb, c0 * 128:c0 * 128 + cw], in_=memV.ap(tb, c0 * 128, c0 * 128 + cw))),
                        R=memV.rg(tb, c0 * 128, c0 * 128 + cw), W=[(f"mv{lp}", tb * 64 + c0, tb * 64 + c0 + cw // 128)],
                        dma=True)
            dense("w_xkv", w_xkv[lp], KC, c.XW, c.XW,
                  lambda k, tb: (xgm.ap(k, tb * 128, (tb + 1) * 128), xgm.rg(k, tb * 128, (tb + 1) * 128)),
                  None, M // 128, evv, "tm")

    def load_layer_consts(lp):
        wpv = w_pool[lp].rearrange("g (k p) d -> p g k d", p=128)
        for g in range(4):
            P.op("pool", (lambda e, g=g: e.dma_start(out=wpool.ap((g * c.PGC, (g + 1) * c.PGC)), in_=wpv[:, g, :, :])),
                 R=[("w_pool", 0, 1)], W=wpool.rg((g * c.PGC, (g + 1) * c.PGC)), dma=True)

    def conv_pool_group(lp, i, banks, n, first_own, kv_only):
        g = i // c.PGC
        w = 2 << g
        off = T - n
        wc = lambda tap: sm_ap(o_wc + (lp * c.NCONV + i) * 3 + tap)
        rr = rstd_row.ap(n0=off, n1=T)
        rr_rg = rstd_row.rg(n0=off, n1=T)
        if first_own:
            P.op("dve", lambda e: e.tensor_scalar(u2.ap(n0=14, n1=16), conv_hist.ap(i, 14, 16), sm_ap(o_hv), None, ALU.mult),
                 R=conv_hist.rg(i) + SM_RG, W=u2.rg(n0=14, n1=16))
            P.op("dve", lambda e: e.tensor_scalar(ubuf.ap(n0=0, n1=16), pool_hist.ap(i), sm_ap(o_hv), None, ALU.mult),
                 R=pool_hist.rg(i) + SM_RG, W=ubuf.rg(n0=0, n1=16))
        elif not kv_only:
            P.op("dve", lambda e: e.tensor_copy(u2.ap(n0=14, n1=16), conv_hist.ap(i, 14, 16)),
                 R=conv_hist.rg(i), W=u2.rg(n0=14, n1=16))
            P.op("dve", lambda e: e.tensor_copy(ubuf.ap(n0=0, n1=16), pool_hist.ap(i)),
                 R=pool_hist.rg(i), W=ubuf.rg(n0=0, n1=16))
        lo = 16 + off
        P.op("dve", lambda e: e.tensor_tensor(cT.ap(n0=0, n1=n), banks[1].ap(n0=0, n1=n), rr, ALU.mult),
             R=banks[1].rg(n0=0, n1=n) + rr_rg, W=cT.rg(n0=0, n1=n))
        P.op("dve", lambda e: e.tensor_tensor(hT.ap(n0=0, n1=n), banks[2].ap(n0=0, n1=n), rr, ALU.mult),
             R=banks[2].rg(n0=0, n1=n) + rr_rg, W=hT.rg(n0=0, n1=n))
        P.op("dve", lambda e: e.tensor_tensor(u2.ap(n0=lo, n1=16 + T), cT.ap(n0=0, n1=n), hT.ap(n0=0, n1=n), ALU.mult),
             R=cT.rg(n0=0, n1=n) + hT.rg(n0=0, n1=n), W=u2.rg(n0=lo, n1=16 + T))
        P.op("dve", lambda e: e.tensor_tensor(ubuf.ap(n0=lo, n1=16 + T), banks[3].ap(n0=0, n1=n), rr, ALU.mult),
             R=banks[3].rg(n0=0, n1=n) + rr_rg, W=ubuf.rg(n0=lo, n1=16 + T))
        if not kv_only:
            P.op("dve", lambda e: e.tensor_tensor(bT.ap(), banks[0].ap(), rr, ALU.mult),
                 R=banks[0].rg() + rr_rg, W=bT.rg())
            P.op("dve", lambda e: e.tensor_scalar(cacc.ap(), u2.ap(n0=16, n1=16 + T), wc(2), None, ALU.mult),
                 R=u2.rg(n0=16, n1=16 + T) + SM_RG, W=cacc.rg())
            P.op("dve", lambda e: e.scalar_tensor_tensor(cacc.ap(), u2.ap(n0=15, n1=15 + T), wc(1), cacc.ap(), ALU.mult, ALU.add),
                 R=u2.rg(n0=15, n1=15 + T) + cacc.rg() + SM_RG, W=cacc.rg())
            P.op("dve", lambda e: e.scalar_tensor_tensor(cacc.ap(), u2.ap(n0=14, n1=14 + T), wc(0), cacc.ap(), ALU.mult, ALU.add),
                 R=u2.rg(n0=14, n1=14 + T) + cacc.rg() + SM_RG, W=cacc.rg())
            P.op("dve", lambda e: e.tensor_tensor(mix.ap(i), bT.ap(), cacc.ap(), ALU.mult),
                 R=bT.rg() + cacc.rg(), W=mix.rg(i))
            src, sh = ubuf, 1
            dsts = [sA, sB]
            k = 0
            while sh < w:
                d = dsts[k % 2]
                P.op("dve", (lambda e, d=d, src=src, sh=sh: e.tensor_tensor(
                    d.ap(n0=sh, n1=16 + T), src.ap(n0=sh, n1=16 + T), src.ap(n0=0, n1=16 + T - sh), ALU.add)),
                    R=src.rg(), W=d.rg(n0=sh, n1=16 + T))
                src = d
                sh *= 2
                k += 1
            if first_own:
                P.op("dve", (lambda e, src=src: e.tensor_tensor(src.ap(n0=16, n1=32), src.ap(n0=16, n1=32),
                                                                 sm_ap(o_corr + g * 16, 16), ALU.mult)),
                     R=src.rg(n0=16, n1=32) + SM_RG, W=src.rg(n0=16, n1=32))
            P.op("dve", (lambda e, src=src: e.scalar_tensor_tensor(
                pooled.ap(i), src.ap(n0=16, n1=16 + T), 1.0 / w, ubuf.ap(n0=16, n1=16 + T), ALU.mult, ALU.subtract)),
                R=src.rg(n0=16, n1=16 + T) + ubuf.rg(n0=16, n1=16 + T), W=pooled.rg(i))
        P.op("dve", lambda e: e.tensor_copy(conv_hist.ap(i, 14, 16), u2.ap(n0=14 + T, n1=16 + T)),
             R=u2.rg(n0=14 + T, n1=16 + T), W=conv_hist.rg(i))
        P.op("dve", lambda e: e.tensor_copy(pool_hist.ap(i), ubuf.ap(n0=T, n1=16 + T)),
             R=ubuf.rg(n0=T, n1=16 + T), W=pool_hist.rg(i))

    def mixer(lp, ti, src, dst, kv_only, first_own):
        t0 = ti * T
        prologue(lp, ti, src, want_col=True)
        chk("pro")
        W = w_in[lp]
        n = 128 if kv_only else T
        for i in range(c.NCONV):
            def ev(c0, nj, banks, cw, i=i):
                conv_pool_group(lp, i, banks, n, first_own, kv_only)
            dense("w_in", W, KC, i * 512, 512, None,
                  lambda k: (xg.ap(k, T - n, T), xg.rg(k, T - n, T), n), 4, ev, "fm")
        if not kv_only:
            for g in range(4):
                bset = state["set"]
                state["set"] ^= 1
                for mc in range(c.PGC):
                    bk = bank[bset * 4 + mc]
                    for kc in range(c.PGC):
                        P.op("pe", (lambda e, bk=bk, g=g, kc=kc, mc=mc: e.matmul(
                            bk.ap(), wpool.ap(g * c.PGC + kc, mc * 128, (mc + 1) * 128), pooled.ap(g * c.PGC + kc),
                            start=(kc == 0), stop=(kc == c.PGC - 1))),
                            R=wpool.rg(g * c.PGC + kc) + pooled.rg(g * c.PGC + kc), W=bk.rg())
                    oc = g * c.PGC + mc
                    P.op("act", (lambda e, bk=bk, oc=oc: e.activation(
                        mix.ap(c.NCONV + c.H + oc), bk.ap(), AF.Copy, scale=sm_ap(o_ps + lp * c.NPOOL + oc))),
                        R=bk.rg() + SM_RG, W=mix.rg(c.NCONV + c.H + oc))
        chk("cp")
        ksv = kscr.rearrange("(h p) t -> p h t", p=128)
        vsv = vscr.rearrange("(b p) n -> p b n", p=128)
        base = c.NCONV * 512
        for hg in range(c.HG):
            cq = base + hg * 1536
            if not kv_only:
                P.op("sp", (lambda e, hg=hg: e.dma_start(out=Kb.ap((0, 4), 0, T), in_=ksv[:, hg * 4:hg * 4 + 4, t0 - T:t0])),
                     R=[("kscr", (ti - 1) * 64 + hg * 4, (ti - 1) * 64 + hg * 4 + 4)], W=Kb.rg((0, 4), 0, T), dma=True)
                P.op("sp", (lambda e, hg=hg: e.dma_start(out=Vb.ap((0, 4)), in_=vsv[:, (ti - 1) * 4:ti * 4, hg * 512:(hg + 1) * 512])),
                     R=[("vscr", (ti - 1) * 64 + hg * 4, (ti - 1) * 64 + hg * 4 + 4)], W=Vb.rg((0, 4)), dma=True)
                dense("w_in", W, KC, cq, 512, None, lambda k: (xg.ap(k), xg.rg(k), T), 4,
                      ev_scaled(lambda oc: (Qb.ap(oc), Qb.rg(oc))), "fm")

            def evk(c0, nj, banks, cw, hg=hg):
                for j in range(nj):
                    bj = banks[j]
                    P.op("dve", (lambda e, j=j, bj=bj: e.tensor_tensor(Kb.ap(j, T, 2 * T), bj.ap(), rstd_row.ap(), ALU.mult)),
                         R=bj.rg() + rstd_row.rg(), W=Kb.rg(j, T, 2 * T))
                P.op("sp", (lambda e: e.dma_start(out=ksv[:, hg * 4:hg * 4 + 4, t0:t0 + T], in_=Kb.ap((0, 4), T, 2 * T))),
                     R=Kb.rg((0, 4), T, 2 * T), W=[("kscr", ti * 64 + hg * 4, ti * 64 + hg * 4 + 4)], dma=True)
            dense("w_in", W, KC, cq + 512, 512, None, lambda k: (xg.ap(k), xg.rg(k), T), 4, evk, "fm")

            def evv(c0, nj, banks, cw, hg=hg):
                for tb in range(nj):
                    bj = banks[tb]
                    P.op("act", (lambda e, tb=tb, bj=bj: e.activation(Vb.ap(4 + tb), bj.ap(), AF.Copy,
                                                                        scale=rstd_col.ap(n0=tb, n1=tb + 1))),
                         R=bj.rg() + rstd_col.rg(), W=Vb.rg(4 + tb))
                P.op("sp", (lambda e: e.dma_start(out=vsv[:, ti * 4:ti * 4 + 4, hg * 512:(hg + 1) * 512], in_=Vb.ap((4, 8)))),
                     R=Vb.rg((4, 8)), W=[("vscr", ti * 64 + hg * 4, ti * 64 + hg * 4 + 4)], dma=True)
            dense("w_in", W, KC, cq + 1024, 512,
                  lambda k, tb: (xg.ap(k, tb * 128, (tb + 1) * 128), xg.rg(k, tb * 128, (tb + 1) * 128)),
                  None, 4, evv, "tm")
            chk("qkv")
            if kv_only:
                continue
            scale = 128.0 ** -0.5
            for hl in range(4):
                h = hg * 4 + hl
                bsb = bias_sb[h % 2]
                P.op("sp", (lambda e, bsb=bsb, h=h: e.dma_start(out=bsb.ap(), in_=biasT[lp, h])),
                     R=[("biasT", 0, 1)], W=bsb.rg(), dma=True)
                ob = bank[4 + (h % 2)]
                db = bank[6 + (h % 2)]
                for qg in range(4):
                    sp_ = scp[qg % 2]
                    pt = PT[qg % 2]
                    for r in range(5):
                        kb = qg + r
                        P.op("pe", (lambda e, sp_=sp_, r=r, kb=kb, hl=hl, qg=qg: e.matmul(
                            sp_.ap(n0=r * 128, n1=(r + 1) * 128), Kb.ap(hl, kb * 128, (kb + 1) * 128),
                            Qb.ap(hl, qg * 128, (qg + 1) * 128), start=True, stop=True)),
                            R=Kb.rg(hl, kb * 128, (kb + 1) * 128) + Qb.rg(hl, qg * 128, (qg + 1) * 128),
                            W=sp_.rg(n0=r * 128, n1=(r + 1) * 128))
                    for (c0_, c1_) in ((0, 512), (512, 640)):
                        P.op("dve", (lambda e, sp_=sp_, bsb=bsb, c0_=c0_, c1_=c1_: e.scalar_tensor_tensor(
                            sct.ap(n0=c0_, n1=c1_), sp_.ap(n0=c0_, n1=c1_), scale, bsb.ap(n0=c0_, n1=c1_),
                            ALU.mult, ALU.add)),
                            R=sp_.rg(n0=c0_, n1=c1_) + bsb.rg(n0=c0_, n1=c1_), W=sct.rg(n0=c0_, n1=c1_))
                    nh = max(0, 4 - qg) * 128 if first_own else 0
                    if nh > 0:
                        P.op("act", (lambda e, pt=pt, nh=nh: e.activation(pt.ap(n0=0, n1=nh), sct.ap(n0=0, n1=nh), AF.Exp,
                                                                            bias=sm_ap(o_neg))),
                             R=sct.rg(n0=0, n1=nh) + SM_RG, W=pt.rg(n0=0, n1=nh))
                    P.op("act", (lambda e, pt=pt, nh=nh: e.activation(pt.ap(n0=nh, n1=640), sct.ap(n0=nh, n1=640), AF.Exp)),
                         R=sct.rg(n0=nh, n1=640), W=pt.rg(n0=nh, n1=640))
                    for r in range(5):
                        kb = qg + r
                        P.op("pe", (lambda e, pt=pt, r=r, kb=kb, hl=hl, qg=qg, ob=ob: e.matmul(
                            ob.ap(n0=qg * 128, n1=(qg + 1) * 128), Vb.ap(kb, hl * 128, (hl + 1) * 128),
                            pt.ap(n0=r * 128, n1=(r + 1) * 128), start=(r == 0), stop=(r == 4))),
                            R=Vb.rg(kb, hl * 128, (hl + 1) * 128) + pt.rg(n0=r * 128, n1=(r + 1) * 128),
                            W=ob.rg(n0=qg * 128, n1=(qg + 1) * 128))
                    for r in range(5):
                        P.op("pe", (lambda e, pt=pt, r=r, qg=qg, db=db: e.matmul(
                            db.ap(n0=qg * 128, n1=(qg + 1) * 128), ones16.ap(),
                            pt.ap(n0=r * 128, n1=(r + 1) * 128), start=(r == 0), stop=(r == 4))),
                            R=ones16.rg() + pt.rg(n0=r * 128, n1=(r + 1) * 128),
                            W=db.rg(n0=qg * 128, n1=(qg + 1) * 128))
                P.op("dve", (lambda e, db=db: e.reciprocal(rec.ap(), db.ap())), R=db.rg(), W=rec.rg())
                P.op("dve", (lambda e, ob=ob, h=h: e.tensor_tensor(mix.ap(c.NCONV + h), ob.ap(), rec.ap(), ALU.mult)),
                     R=ob.rg() + rec.rg(), W=mix.rg(c.NCONV + h))
        if kv_only:
            return
        chk("att")
        dense("w_out", w_out[lp], KC, 0, c.D, None, lambda k: (mix.ap(k), mix.rg(k), T), 4, post_evac("mix"), "fm")
        chk("wout")
        tail(lp, ti, src, dst, 1, 2)

    def xattn(lp, ti, src, dst):
        MB = c.MEM // 128
        mkv = mkscr[lp].rearrange("(c p) t -> p c t", p=128)
        mvv = mvscr[lp].rearrange("(b p) n -> p b n", p=128)
        P.op("sp", lambda e: e.dma_start(out=memK.ap((0, XC)), in_=mkv), R=[(f"mk{lp}", 0, 1 << 20)], W=memK.rg((0, XC)), dma=True)
        P.op("sp", lambda e: e.dma_start(out=memV.ap((0, MB)), in_=mvv), R=[(f"mv{lp}", 0, 1 << 20)], W=memV.rg((0, MB)), dma=True)
        dense("w_xq", w_xq[lp], KC, 0, c.XW, None, lambda k: (xg.ap(k), xg.rg(k), T), 4,
              ev_scaled(lambda oc: (qx.ap(oc), qx.rg(oc))), "fm")
        sc = 256.0 ** -0.5
        for hx in range(c.XH):
            ptx = PTx[hx % 2]
            sb_ = [bank[(hx % 2) * 2 + mb] for mb in range(MB)]
            for mb in range(MB):
                for dc in range(2):
                    P.op("pe", (lambda e, mb=mb, dc=dc, hx=hx, sb_=sb_: e.matmul(
                        sb_[mb].ap(), memK.ap(2 * hx + dc, mb * 128, (mb + 1) * 128), qx.ap(2 * hx + dc),
                        start=(dc == 0), stop=(dc == 1))),
                        R=memK.rg(2 * hx + dc, mb * 128, (mb + 1) * 128) + qx.rg(2 * hx + dc), W=sb_[mb].rg())
                P.op("act", (lambda e, mb=mb, ptx=ptx, sb_=sb_: e.activation(ptx.ap(mb), sb_[mb].ap(), AF.Exp, scale=sc)),
                     R=sb_[mb].rg(), W=ptx.rg(mb))
            db = bank[6 + hx % 2]
            for mb in range(MB):
                P.op("pe", (lambda e, mb=mb, ptx=ptx, db=db: e.matmul(db.ap(), ones16.ap(), ptx.ap(mb),
                                                                      start=(mb == 0), stop=(mb == MB - 1))),
                     R=ones16.rg() + ptx.rg(mb), W=db.rg())
            P.op("dve", (lambda e, db=db: e.reciprocal(rec.ap(), db.ap())), R=db.rg(), W=rec.rg())
            for dc in range(2):
                ob = bank[4 + dc]
                for mb in range(MB):
                    P.op("pe", (lambda e, mb=mb, dc=dc, ptx=ptx, ob=ob, hx=hx: e.matmul(
                        ob.ap(), memV.ap(mb, hx * 256 + dc * 128, hx * 256 + (dc + 1) * 128), ptx.ap(mb),
                        start=(mb == 0), stop=(mb == MB - 1))),
                        R=memV.rg(mb, hx * 256 + dc * 128, hx * 256 + (dc + 1) * 128) + ptx.rg(mb), W=ob.rg())
                P.op("dve", (lambda e, ob=ob, dc=dc, hx=hx: e.tensor_tensor(ox.ap(2 * hx + dc), ob.ap(), rec.ap(), ALU.mult)),
                     R=ob.rg() + rec.rg(), W=ox.rg(2 * hx + dc))
        dense("w_xo", w_xo[lp], XC, 0, c.D, None, lambda k: (ox.ap(k), ox.rg(k), T), 4, post_evac("x"), "fm")
        tail(lp, ti, src, dst, 3, 4)

    def ffn(lp, ti, src, dst):
        def evh(c0, nj, banks, cw):
            jg = c0 // 4
            for j in range(2):
                hc = jg * 2 + j
                a, b_, s_ = t1[j], t2[j], sg[j]
                gb, ub = banks[j], banks[2 + j]
                P.op("dve", (lambda e, a=a, gb=gb: e.tensor_tensor(a.ap(), gb.ap(), rstd_row.ap(), ALU.mult)),
                     R=gb.rg() + rstd_row.rg(), W=a.rg())
                P.op("act", (lambda e, a=a, s_=s_: e.activation(s_.ap(), a.ap(), AF.Silu)), R=a.rg(), W=s_.rg())
                P.op("dve", (lambda e, b_=b_, ub=ub: e.tensor_tensor(b_.ap(), ub.ap(), rstd_row.ap(), ALU.mult)),
                     R=ub.rg() + rstd_row.rg(), W=b_.rg())
                P.op("dve", (lambda e, b_=b_, s_=s_, hc=hc: e.tensor_tensor(hb.ap(hc), s_.ap(), b_.ap(), ALU.mult)),
                     R=s_.rg() + b_.rg(), W=hb.rg(hc))
        dense("w_gu", w_gu[lp], KC, 0, 2 * c.DFF, None, lambda k: (xg.ap(k), xg.rg(k), T), 4, evh, "fm")
        dense("w_down", w_down[lp], FC, 0, c.D, None, lambda k: (hb.ap(k), hb.rg(k), T), 4, post_evac("f"), "fm")
        tail(lp, ti, src, dst, 5, None)

    def schedule():
        chk("init")
        mem_prologue()
        chk("mem")
        for pos, lp in enumerate(range(LP)):
            load_layer_consts(lp)
            kv_tile = OWN0 - (LP - pos)
            last = (pos == LP - 1)
            for ti in range(kv_tile, NLOC):
                kv_only = (ti == kv_tile)
                first_own = (ti == OWN0)
                src = xsrc(pos, ti)
                dst_mid = (xs, "xs", NTOK, ti * T)
                mixer(lp, ti, src, dst_mid, kv_only, first_own)
                chk("kv" if kv_only else "mixer")
                if kv_only:
                    continue
                xattn(lp, ti, dst_mid, dst_mid)
                chk("xattn")
                if last and last_writes_output:
                    assert ti >= OWN0
                    dst = (x_out, "x_out", c.NT_OWN * T, (ti - OWN0) * T)
                else:
                    dst = dst_mid
                ffn(lp, ti, dst_mid, dst)
                chk("ffn")

    _CNT.clear()
    try:
        schedule()
    except _Stop:
        pass
    P.emit(nc, es)
    es.close()
    return nc


def _bias_table(rel_bias_l, H):
    a = np.arange(2)[:, None, None, None, None]
    kk = np.arange(64)[None, :, None, None, None]
    r = np.arange(5)[None, None, :, None, None]
    b = np.arange(2)[None, None, None, :, None]
    qq = np.arange(64)[None, None, None, None, :]
    d = 8 - 2 * r + b - a
    rel = np.clip(d * 64 + qq - kk, -256, 256) + 256
    valid = (d >= 0) & (d <= 8)
    rel = np.broadcast_to(rel, (2, 64, 5, 2, 64)).reshape(128, 640)
    valid = np.broadcast_to(valid, (2, 64, 5, 2, 64)).reshape(128, 640)
    tab = rel_bias_l[:, rel]
    return np.where(valid[None], tab, np.float32(NEG)).astype(np.float32)


def _col_perm(c):
    idx = []
    for i in range(c.NCONV):
        for base in (0, c.CW, 2 * c.CW, 3 * c.CW + 3 * c.AW):
            idx.append(np.arange(base + i * 128, base + (i + 1) * 128))
    for hg in range(c.HG):
        for base in (3 * c.CW, 3 * c.CW + c.AW, 3 * c.CW + 2 * c.AW):
            idx.append(np.arange(base + hg * 512, base + (hg + 1) * 512))
    return np.concatenate(idx)


def _pm(v):
    v = np.asarray(v, np.float32)
    lead = v.shape[:-1]
    k = v.shape[-1] // 128
    v = v.reshape(lead + (k, 128))
    return np.moveaxis(v, -1, 0).reshape(128, -1)


_NC_CACHE = {}


def run_layers(c, layers, xT_locals, inputs, memT):
    LP = len(layers)
    key = (LP, c.D, c.NT_OWN)
    if key not in _NC_CACHE:
        _NC_CACHE[key] = build_program(c, list(range(LP)))
    nc = _NC_CACHE[key]
    ls = list(layers)
    f = np.float32
    perm = _col_perm(c)
    w_in = np.stack([np.take(np.asarray(inputs["w_in"][l], f), perm, axis=1) for l in ls])
    nj = c.DFF // 256
    w_gu = np.stack([np.stack([np.asarray(inputs["w_gate"][l], f).reshape(c.D, nj, 256),
                               np.asarray(inputs["w_up"][l], f).reshape(c.D, nj, 256)], axis=2).reshape(c.D, 2 * c.DFF)
                     for l in ls])
    shared = {
        "memT": memT,
        "w_in": w_in, "w_gu": w_gu,
        "w_down": np.ascontiguousarray(np.asarray(inputs["w_down"], f)[ls]),
        "w_out": np.ascontiguousarray(np.asarray(inputs["w_out"], f)[ls]),
        "w_xq": np.ascontiguousarray(np.asarray(inputs["w_xq"], f)[ls]),
        "w_xkv": np.ascontiguousarray(np.asarray(inputs["w_xkv"], f)[ls]),
        "w_xo": np.ascontiguousarray(np.asarray(inputs["w_xo"], f)[ls]),
        "w_pool": np.ascontiguousarray(np.asarray(inputs["w_pool"], f)[ls]),
        "biasT": np.stack([_bias_table(np.asarray(inputs["rel_bias"][l], f), c.H) for l in ls]),
    }
    gains = np.stack([np.stack([np.asarray(inputs[k][l], f) for k in
                                ("g_mix_pre", "g_mix_post", "g_x_pre", "g_x_post", "g_ffn_pre", "g_ffn_post")])
                      for l in ls])
    wconv = np.stack([np.asarray(inputs["w_conv"][l], f).T for l in ls])
    in_maps = []
    for core in range(NCORES):
        negv = np.float32(NEG if core == 0 else 0.0)
        hv = np.float32(0.0 if core == 0 else 1.0)
        corr = np.ones((4, 16), f)
        if core == 0:
            for g in range(4):
                w = 2 << g
                for t in range(16):
                    corr[g, t] = np.float32(w) / np.float32(min(t + 1, w))
        parts = [
            _pm(gains),
            _pm(np.asarray(inputs["g_mem"], f)),
            np.moveaxis(wconv.reshape(LP, c.NCONV, 128, 3), 2, 0).reshape(128, -1),
            _pm(np.stack([np.asarray(inputs["pool_scale"][l], f) for l in ls])),
            np.full((128, 1), negv, f), np.full((128, 1), hv, f),
            np.broadcast_to(corr.reshape(1, 64), (128, 64)),
        ]
        small = np.ascontiguousarray(np.concatenate(parts, axis=1), dtype=f)
        m = dict(shared)
        m["x_in"] = xT_locals[core]
        m["small"] = small
        in_maps.append(m)
    res = run_bass_kernel_spmd(nc, in_maps, core_ids=list(range(NCORES)))
    return [r["x_out"] for r in res.results]


def kernel_cfg(c, fused, **inputs):
    f = np.float32
    x = np.asarray(inputs["x"], f)[0]
    SEQ = x.shape[0]
    own = c.NT_OWN * T
    assert SEQ == NCORES * own
    memT = np.ascontiguousarray(np.asarray(inputs["mem"], f)[0].T)
    groups = [list(range(c.L))] if fused else [[l] for l in range(c.L)]
    xT = np.ascontiguousarray(x.T)
    for ls in groups:
        halo = len(ls) * T
        xpad = np.concatenate([np.zeros((c.D, halo), f), xT], axis=1)
        locs = [np.ascontiguousarray(xpad[:, core * own: core * own + halo + own]) for core in range(NCORES)]
        outs = run_layers(c, ls, locs, inputs, memT)
        xT = np.concatenate(outs, axis=1)
    return np.ascontiguousarray(xT.T)[None].astype(f)


FUSED = False


def kernel(**inputs):
    return kernel_cfg(Cfg(), FUSED, **inputs)
```
